# Optimizing a Trainium2 kernel written in Bass

```python
import jax, jax.numpy as jnp
from jax import lax
import numpy as np

D_MODEL = 2048
BATCH = 4
SEQ = 2048
DEPTH = 4

N_EVEN = (DEPTH + 1) // 2
N_ODD = DEPTH // 2
D_FF = 5632
NORM_EPS = 1e-6
LN_EPS = 1e-5
POOL_DIM = D_MODEL // 2
POOL_WINDOWS = (2, 4, 8, 16)
N_POOL_GROUPS = len(POOL_WINDOWS)
POOL_GROUP_DIM = POOL_DIM // N_POOL_GROUPS
RW_DIM = D_MODEL // 2
RW_HEAD = 64
RW_HEADS = RW_DIM // RW_HEAD
RW_DECAY_LORA = 64
RW_A_LORA = 64
RW_GATE_LORA = 160
RW_IN = 3 * RW_DIM + RW_DECAY_LORA + RW_A_LORA + RW_GATE_LORA
EV_IN = POOL_DIM + RW_IN
RW_GN_EPS = 64e-5
CV_DIM = D_MODEL // 2
CV_WIDTH = 31
SG_DIM = D_MODEL // 2
SG_CHUNK = 128
SG_GROUPS = 8
SG_GROUP_DIM = SG_DIM // SG_GROUPS
OD_IN = 2 * CV_DIM + 2 * SG_DIM

kernel_name = 'hybrid_pool_rwkv7_conv_gmlp_macaron'


def rms_norm(x, g):
    xf = x.astype(jnp.float32)
    y = xf * lax.rsqrt(jnp.mean(jnp.square(xf), -1, keepdims=True) + NORM_EPS)
    return (y * g).astype(x.dtype)


def layer_norm(x, g, b):
    xf = x.astype(jnp.float32)
    mu = jnp.mean(xf, -1, keepdims=True)
    var = jnp.mean(jnp.square(xf - mu), -1, keepdims=True)
    return (xf - mu) * lax.rsqrt(var + LN_EPS) * g + b


def swiglu(x, w1, w3, w2):
    return (jax.nn.silu(x @ w1) * (x @ w3)) @ w2


def multiscale_pool(p, w_grp, scale):
    B, T, _ = p.shape
    pg = p.astype(jnp.float32).reshape(B, T, N_POOL_GROUPS, POOL_GROUP_DIM)
    cs = jnp.cumsum(pg, axis=1)
    t = jnp.arange(T)
    means = []
    for gi, w in enumerate(POOL_WINDOWS):
        c = cs[:, :, gi]
        c_prev = jnp.pad(c, ((0, 0), (w, 0), (0, 0)))[:, :T]
        cnt = jnp.minimum(t + 1, w).astype(jnp.float32)
        means.append((c - c_prev) / cnt[None, :, None])
    d = jnp.stack(means, axis=2) - pg
    out = jnp.einsum('btgc,gcd->btgd', d, w_grp)
    return out.reshape(B, T, POOL_DIM) * scale


def wkv7_scan(r, decay, k, v, a_vec, b_vec):
    B, T, H, N = r.shape

    def step(S, inp):
        r_t, w_t, k_t, v_t, a_t, b_t = inp
        sa = jnp.einsum('bhij,bhj->bhi', S, a_t)
        S = S * w_t[:, :, None, :] + sa[..., None] * b_t[:, :, None, :] + v_t[..., None] * k_t[:, :, None, :]
        y = jnp.einsum('bhij,bhj->bhi', S, r_t)
        return S, y

    xs = tuple(jnp.moveaxis(z, 1, 0) for z in (r, decay, k, v, a_vec, b_vec))
    S0 = jnp.zeros((B, H, N, N), jnp.float32)
    _, y = lax.scan(step, S0, xs)
    return jnp.moveaxis(y, 0, 1)


def rwkv7_mix(h, mu, w0, w2, a0, a2, g2, k_k, k_a, r_k, ln_g, ln_b):
    B, T, _ = h.shape
    h = h.astype(jnp.float32)
    prev = jnp.pad(h, ((0, 0), (1, 0), (0, 0)))[:, :T]
    h = h + (prev - h) * mu
    cuts = [RW_DIM, 2 * RW_DIM, 3 * RW_DIM, 3 * RW_DIM + RW_DECAY_LORA, 3 * RW_DIM + RW_DECAY_LORA + RW_A_LORA]
    r, k, v, hw, ha, hg = jnp.split(h, cuts, axis=-1)
    w_log = -jax.nn.softplus(-(w0 + jnp.tanh(hw) @ w2)) - 0.5
    decay = jnp.exp(-jnp.exp(w_log))
    a = jax.nn.sigmoid(a0 + ha @ a2)
    g = jax.nn.sigmoid(hg) @ g2
    heads = lambda z: z.reshape(B, T, RW_HEADS, RW_HEAD)
    kk = heads(k * k_k)
    kk = kk / jnp.maximum(jnp.sqrt(jnp.sum(jnp.square(kk), -1, keepdims=True)), 1e-12)
    k = k * (1.0 + (a - 1.0) * k_a)
    rh, kh, vh = heads(r), heads(k), heads(v)
    y = wkv7_scan(rh, heads(decay), kh, vh, -kk, kk * heads(a))
    ym = jnp.mean(y, -1, keepdims=True)
    yv = jnp.mean(jnp.square(y - ym), -1, keepdims=True)
    y = ((y - ym) * lax.rsqrt(yv + RW_GN_EPS)).reshape(B, T, RW_DIM) * ln_g + ln_b
    bonus = jnp.sum(rh * kh * r_k, -1, keepdims=True) * vh
    y = y + bonus.reshape(B, T, RW_DIM)
    return y * g


def conformer_conv(c, dw, db, ln_g, ln_b):
    c = c.astype(jnp.float32)
    a, gate = jnp.split(c, 2, axis=-1)
    y = a * jax.nn.sigmoid(gate)
    y = lax.conv_general_dilated(y, dw.astype(jnp.float32)[:, None, :], (1,), ((CV_WIDTH - 1, 0),),
                                 dimension_numbers=('NWC', 'WIO', 'NWC'), feature_group_count=CV_DIM) + db
    return jax.nn.silu(layer_norm(y, ln_g, ln_b))


def chunked_spatial_gate(z, ln_g, ln_b, w_s, b_s):
    z = jax.nn.gelu(z.astype(jnp.float32), approximate=False)
    u, v = jnp.split(z, 2, axis=-1)
    v = layer_norm(v, ln_g, ln_b)
    B, T, _ = v.shape
    vc = v.reshape(B, T // SG_CHUNK, SG_CHUNK, SG_GROUPS, SG_GROUP_DIM)
    mask = jnp.tril(jnp.ones((SG_CHUNK, SG_CHUNK), bool))
    ws = jnp.where(mask[None], w_s, 0.0)
    s = jnp.einsum('gij,bcjgd->bcigd', ws, vc) + b_s.T[None, None, :, :, None]
    return u * s.reshape(B, T, SG_DIM)


def setup_inputs(seed: int = 0) -> dict:
    key = jax.random.key(seed)
    ks = jax.random.split(key, 32)
    f32 = jnp.float32
    nrm = lambda k, shape, scale: jax.random.normal(k, shape, f32) * scale
    return {
        'x': nrm(ks[0], (BATCH, SEQ, D_MODEL), 1.0),
        'norm_g': 1.0 + nrm(ks[1], (DEPTH, 6, D_MODEL), 0.02),
        'ffn_w1': nrm(ks[2], (DEPTH, 2, D_MODEL, D_FF), D_MODEL ** -0.5),
        'ffn_w3': nrm(ks[3], (DEPTH, 2, D_MODEL, D_FF), D_MODEL ** -0.5),
        'ffn_w2': nrm(ks[4], (DEPTH, 2, D_FF, D_MODEL), D_FF ** -0.5),
        'ev_w_in': nrm(ks[5], (N_EVEN, D_MODEL, EV_IN), D_MODEL ** -0.5),
        'ev_mu': jax.random.uniform(ks[6], (N_EVEN, RW_IN), f32, 0.0, 1.0),
        'pool_w': nrm(ks[7], (N_EVEN, N_POOL_GROUPS, POOL_GROUP_DIM, POOL_GROUP_DIM), POOL_GROUP_DIM ** -0.5),
        'pool_scale': 1.0 + nrm(ks[8], (N_EVEN, POOL_DIM), 0.1),
        'rw_w0': jax.random.uniform(ks[9], (N_EVEN, RW_DIM), f32, -6.5, -1.0),
        'rw_w2': nrm(ks[10], (N_EVEN, RW_DECAY_LORA, RW_DIM), RW_DECAY_LORA ** -0.5),
        'rw_a0': nrm(ks[11], (N_EVEN, RW_DIM), 0.1),
        'rw_a2': nrm(ks[12], (N_EVEN, RW_A_LORA, RW_DIM), RW_A_LORA ** -0.5),
        'rw_g2': nrm(ks[13], (N_EVEN, RW_GATE_LORA, RW_DIM), RW_GATE_LORA ** -0.5),
        'rw_kk': 0.85 + nrm(ks[14], (N_EVEN, RW_DIM), 0.02),
        'rw_ka': 1.0 + nrm(ks[15], (N_EVEN, RW_DIM), 0.02),
        'rw_rk': -0.04 + nrm(ks[16], (N_EVEN, RW_HEADS, RW_HEAD), 0.02),
        'rw_ln_g': 1.0 + nrm(ks[17], (N_EVEN, RW_DIM), 0.02),
        'rw_ln_b': nrm(ks[18], (N_EVEN, RW_DIM), 0.02),
        'ev_w_out': nrm(ks[19], (N_EVEN, POOL_DIM + RW_DIM, D_MODEL), (POOL_DIM + RW_DIM) ** -0.5),
        'od_w_in': nrm(ks[20], (N_ODD, D_MODEL, OD_IN), D_MODEL ** -0.5),
        'cv_dw': nrm(ks[21], (N_ODD, CV_WIDTH, CV_DIM), CV_WIDTH ** -0.5),
        'cv_db': nrm(ks[22], (N_ODD, CV_DIM), 0.02),
        'cv_ln_g': 1.0 + nrm(ks[23], (N_ODD, CV_DIM), 0.02),
        'cv_ln_b': nrm(ks[24], (N_ODD, CV_DIM), 0.02),
        'sg_ln_g': 1.0 + nrm(ks[25], (N_ODD, SG_DIM), 0.02),
        'sg_ln_b': nrm(ks[26], (N_ODD, SG_DIM), 0.02),
        'sg_ws': nrm(ks[27], (N_ODD, SG_GROUPS, SG_CHUNK, SG_CHUNK), SG_CHUNK ** -0.5),
        'sg_b': 1.0 + nrm(ks[28], (N_ODD, SG_GROUPS, SG_CHUNK), 0.02),
        'od_w_out': nrm(ks[29], (N_ODD, CV_DIM + SG_DIM, D_MODEL), (CV_DIM + SG_DIM) ** -0.5),
    }


def reference(x, norm_g, ffn_w1, ffn_w3, ffn_w2, ev_w_in, ev_mu, pool_w, pool_scale, rw_w0, rw_w2,
              rw_a0, rw_a2, rw_g2, rw_kk, rw_ka, rw_rk, rw_ln_g, rw_ln_b, ev_w_out, od_w_in, cv_dw,
              cv_db, cv_ln_g, cv_ln_b, sg_ln_g, sg_ln_b, sg_ws, sg_b, od_w_out):
    h = x
    for layer in range(DEPTH):
        g = norm_g[layer]
        f = swiglu(rms_norm(h, g[0]), ffn_w1[layer, 0], ffn_w3[layer, 0], ffn_w2[layer, 0])
        h = h + 0.5 * rms_norm(f, g[1]).astype(h.dtype)
        z = rms_norm(h, g[2])
        if layer % 2 == 0:
            e = layer // 2
            p = z @ ev_w_in[e]
            m = jnp.concatenate([
                multiscale_pool(p[..., :POOL_DIM], pool_w[e], pool_scale[e]),
                rwkv7_mix(p[..., POOL_DIM:], ev_mu[e], rw_w0[e], rw_w2[e], rw_a0[e], rw_a2[e], rw_g2[e],
                          rw_kk[e], rw_ka[e], rw_rk[e], rw_ln_g[e], rw_ln_b[e]),
            ], axis=-1)
            m = m @ ev_w_out[e]
        else:
            o = layer // 2
            p = z @ od_w_in[o]
            m = jnp.concatenate([
                conformer_conv(p[..., :2 * CV_DIM], cv_dw[o], cv_db[o], cv_ln_g[o], cv_ln_b[o]),
                chunked_spatial_gate(p[..., 2 * CV_DIM:], sg_ln_g[o], sg_ln_b[o], sg_ws[o], sg_b[o]),
            ], axis=-1)
            m = m @ od_w_out[o]
        h = h + rms_norm(m, g[3]).astype(h.dtype)
        f = swiglu(rms_norm(h, g[4]), ffn_w1[layer, 1], ffn_w3[layer, 1], ffn_w2[layer, 1])
        h = h + 0.5 * rms_norm(f, g[5]).astype(h.dtype)
    return h
```

```python
import numpy as np
import concourse.bass as bass
import concourse.mybir as mybir
from concourse.bass_utils import run_bass_kernel_spmd

F32 = mybir.dt.float32
BF16 = mybir.dt.bfloat16
AF = mybir.ActivationFunctionType
ALU = mybir.AluOpType
AX = mybir.AxisListType

EPOCH = 30000


class Buf:
    def __init__(self, name):
        self.name = name
        self.last_write = None
        self.reads = []
        self.ld_sem = None
        self.ld_cnt = 0
        self.st_sem = None
        self.st_cnt = 0
        self.const = False
        self.pend = False
        self.excl = False


class Prog:
    def __init__(self, nc, stack):
        self.nc = nc
        self.stack = stack
        self.eng = {"pe": nc.tensor, "act": nc.scalar, "dve": nc.vector, "pool": nc.gpsimd, "sp": nc.sync}
        self.cnt = {k: 0 for k in self.eng}
        self.sems = {k: [] for k in self.eng}
        self.known = {k: {} for k in self.eng}
        self.nsem = 0
        self.out_tokens = []
        self.pending = {k: [] for k in self.eng}
        self.dma_tokens = {}

    def new_sem(self, name):
        self.nsem += 1
        return self.stack.enter_context(self.nc.semaphore(f"{name}_{self.nsem}"))

    def _progress_token(self, e):
        n = self.cnt[e]
        ep = n // EPOCH
        while len(self.sems[e]) <= ep:
            self.sems[e].append(self.new_sem(f"pg_{e}"))
        self.cnt[e] = n + 1
        return (self.sems[e][ep], n % EPOCH + 1)

    def _wait(self, e, tok):
        sem, val = tok
        k = self.known[e]
        key = id(sem)
        if k.get(key, 0) >= val:
            return
        k[key] = val
        self.eng[e].wait_ge(sem, val)

    def _collect(self, reads, writes, e=None):
        waits = []
        for b in list(reads) + list(writes):
            assert (not b.pend) or b.pend == e, f"dependency on pending buffer {b.name}"
        for b in reads:
            if b.last_write is not None:
                waits.append(b.last_write)
            if b.excl:
                waits.extend(b.reads)
        for b in writes:
            if b.last_write is not None:
                waits.append(b.last_write)
            waits.extend(b.reads)
        return waits

    @staticmethod
    def _add_read(b, tok):
        for i, t in enumerate(b.reads):
            if t[0] is tok[0]:
                if t[1] < tok[1]:
                    b.reads[i] = tok
                return
        b.reads.append(tok)

    def barrier(self):
        toks = []
        for e in ("pe", "act", "dve", "pool"):
            assert not self.pending[e]
            if self.cnt[e] > 0:
                n = self.cnt[e] - 1
                toks.append((self.sems[e][n // EPOCH], n % EPOCH + 1))
        toks.extend(self.dma_tokens.values())
        for e in ("pe", "act", "dve", "pool", "sp"):
            for t in toks:
                self._wait(e, t)

    def op(self, e, fn, reads=(), writes=(), inc=True):
        reads = [b for b in reads if b is not None]
        writes = [b for b in writes if b is not None]
        for tok in self._collect(reads, writes, e):
            self._wait(e, tok)
        ins = fn(self.eng[e])
        pend = self.pending[e]
        pend.append((reads, writes))
        if not inc:
            for b in list(reads) + list(writes):
                b.pend = e
            return None
        tok = self._progress_token(e)
        ins.then_inc(tok[0], 1)
        for (rs, ws) in pend:
            for b in rs:
                b.pend = False
                if not b.const:
                    self._add_read(b, tok)
            for b in ws:
                b.pend = False
                b.last_write = tok
                b.reads = []
        self.pending[e] = []
        return tok

    def dma(self, e, out_ap, in_ap, reads=(), writes=(), sem_buf=None, kind="ld", is_output=False, **kw):
        for tok in self._collect(reads, writes):
            self._wait(e, tok)
        ins = self.eng[e].dma_start(out=out_ap, in_=in_ap, **kw)
        if kind == "ld":
            if sem_buf.ld_sem is None:
                sem_buf.ld_sem = self.new_sem("ld")
            sem_buf.ld_cnt += 16
            tok = (sem_buf.ld_sem, sem_buf.ld_cnt)
        else:
            if sem_buf.st_sem is None:
                sem_buf.st_sem = self.new_sem("st")
            sem_buf.st_cnt += 16
            tok = (sem_buf.st_sem, sem_buf.st_cnt)
        ins.then_inc(tok[0], 16)
        self.dma_tokens[id(tok[0])] = tok
        for b in reads:
            if not b.const:
                self._add_read(b, tok)
        for b in writes:
            b.last_write = tok
            b.reads = []
        if is_output:
            self.out_tokens.append(tok)
        return tok

    def finish(self):
        last = {}
        for tok in self.out_tokens:
            if id(tok[0]) not in last or last[id(tok[0])][1] < tok[1]:
                last[id(tok[0])] = tok
        for tok in last.values():
            self._wait("sp", tok)
        for e in ("pe", "act", "dve", "pool"):
            if self.cnt[e] > 0:
                n = self.cnt[e] - 1
                self._wait("sp", (self.sems[e][n // EPOCH], n % EPOCH + 1))

    def sbuf(self, name, shape, dt):
        t = self.stack.enter_context(self.nc.sbuf_tensor("sb_" + name, list(shape), dt))
        return t

    def psum(self, name, shape, dt=F32):
        t = self.stack.enter_context(self.nc.psum_tensor("pp_" + name, list(shape), dt))
        return t


D = 2048
DFF = 5632
KC = D // 128
FC = DFF // 128
TOK = 1024
TB = 512
NTB = TOK // TB
NORM_EPS = 1e-6
NSLOT = 6


def tile_w(W, pad_to=None):
    K, N = W.shape
    if pad_to is not None and N < pad_to:
        W = np.concatenate([W, np.zeros((K, pad_to - N), W.dtype)], axis=1)
        N = pad_to
    return np.ascontiguousarray(
        W.reshape(K // 128, 128, N // 128, 128).transpose(2, 1, 0, 3).reshape(N // 128, 128, K))


class WStream:
    def __init__(self, P, nslots=NSLOT):
        self.P = P
        self.slots = [P.sbuf(f"wsl{i}", [128, 2048], BF16) for i in range(nslots)]
        self.bufs = [Buf(f"wsl{i}") for i in range(nslots)]
        self.plan = []
        self.nload = 0
        self.nuse = 0

    def add(self, ap, ncols):
        self.plan.append((ap, ncols))

    def _prefetch(self, upto):
        upto = min(upto, len(self.plan))
        while self.nload < upto:
            i = self.nload
            s = i % len(self.slots)
            ap, ncols = self.plan[i]
            self.P.dma("pool", self.slots[s][:, 0:ncols], ap, writes=[self.bufs[s]], sem_buf=self.bufs[s])
            self.nload += 1

    def get(self):
        i = self.nuse
        self._prefetch(i + len(self.slots) - 2)
        self.nuse += 1
        s = i % len(self.slots)
        return self.slots[s], self.bufs[s]


class KB:
    def __init__(self, nc, stack):
        self.nc = nc
        self.stack = stack
        self.P = Prog(nc, stack)
        self.in_names = []
        self.out_names = []

    def din(self, name, shape, dt=F32):
        self.in_names.append(name)
        return self.nc.dram_tensor(name, list(shape), dt, kind="ExternalInput").ap()

    def dout(self, name, shape, dt=F32):
        self.out_names.append(name)
        return self.nc.dram_tensor(name, list(shape), dt, kind="ExternalOutput").ap()

    def setup_common(self):
        P = self.P
        self.ps = [P.psum(f"ps{i}", [128, 512]) for i in range(7)]
        self.pstb = P.psum("pstb", [128, 1024], BF16)
        self.bps = [Buf(f"ps{i}") for i in range(8)]
        for b in self.bps:
            b.excl = True
        self.bpst = self.bps[7]
        self.ones_f = P.sbuf("ones_f", [128, 128], F32)
        self.bones = Buf("ones")
        P.op("dve", lambda e: e.memset(self.ones_f[:], 1.0), writes=[self.bones])
        self.bones.const = True
        self.xT = P.sbuf("xT", [128, KC, TOK], F32)
        self.bx = [Buf(f"x{t}") for t in range(NTB)]
        self.gcol = P.sbuf("gcol", [128, 2 * 6 * KC], F32)
        self.bg = Buf("gcol")
        self.ws = WStream(P)
        self.rs = P.sbuf("rs", [128, TB], F32)
        self.brs = Buf("rs")
        self.sq = [P.sbuf(f"sq{i}", [128, TB], F32) for i in range(2)]
        self.bsq = [Buf(f"sq{i}") for i in range(2)]
        self.nsq = 0

    def load_x(self, x_ap, g_ap):
        P = self.P
        P.dma("sp", self.gcol[:], g_ap, writes=[self.bg], sem_buf=self.bg)
        for tb in range(NTB):
            for kc in range(KC):
                P.dma("sp" if kc % 2 == 0 else "act", self.xT[:, kc, tb * TB:(tb + 1) * TB],
                      x_ap[kc * 128:(kc + 1) * 128, tb * TB:(tb + 1) * TB], writes=[self.bx[tb]], sem_buf=self.bx[tb])

    def store_x(self, y_ap):
        P = self.P
        for tb in range(NTB):
            for kc in range(KC):
                P.dma("sp", y_ap[kc * 128:(kc + 1) * 128, tb * TB:(tb + 1) * TB],
                      self.xT[:, kc, tb * TB:(tb + 1) * TB], reads=[self.bx[tb]], sem_buf=self.bx[tb], kind="st",
                      is_output=True)

    def stat_acc(self, src_ap, src_bufs, first, last):
        P = self.P
        i = self.nsq % 2
        self.nsq += 1
        sq, bsq = self.sq[i], self.bsq[i]
        P.op("act", lambda e: e.activation(out=sq[:], in_=src_ap, func=AF.Square), reads=src_bufs, writes=[bsq])
        P.op("pe", lambda e: e.matmul(self.ps[6][:], lhsT=self.ones_f[:], rhs=sq[:], start=first, stop=last),
             reads=[bsq, self.bones], writes=[self.bps[6]])

    def stat_finish(self, n_feat, eps):
        P = self.P
        P.op("dve", lambda e: e.tensor_scalar(out=self.rs[:], in0=self.ps[6][:], scalar1=1.0 / n_feat, scalar2=eps,
                                               op0=ALU.mult, op1=ALU.add), reads=[self.bps[6]], writes=[self.brs])
        P.op("act", lambda e: e.activation(out=self.rs[:], in_=self.rs[:], func=AF.Sqrt), reads=[self.brs],
             writes=[self.brs])
        P.op("dve", lambda e: e.reciprocal(out=self.rs[:], in_=self.rs[:]), reads=[self.brs], writes=[self.brs])

    def prenorm(self, l, n, tb, zT, bz):
        P = self.P
        blk = slice(tb * TB, (tb + 1) * TB)
        for kc in range(KC):
            self.stat_acc(self.xT[:, kc, blk], [self.bx[tb]], kc == 0, kc == KC - 1)
        self.stat_finish(D, NORM_EPS)
        gi = (l * 6 + n) * KC
        for kc in range(KC):
            P.op("dve", lambda e: e.scalar_tensor_tensor(out=zT[:, kc, :], in0=self.xT[:, kc, blk],
                                                          scalar=self.gcol[:, gi + kc:gi + kc + 1], in1=self.rs[:],
                                                          op0=ALU.mult, op1=ALU.mult),
                 reads=[self.bx[tb], self.brs, self.bg], writes=[bz])

    def postnorm_add(self, l, n, tb, fs, bfs, tmp, btmp):
        P = self.P
        blk = slice(tb * TB, (tb + 1) * TB)
        self.stat_finish(D, NORM_EPS)
        for dc in range(KC):
            t, bt = tmp[dc % 2], btmp[dc % 2]
            P.op("dve", lambda e: e.tensor_tensor(out=t[:], in0=fs[:, dc, :], in1=self.rs[:], op=ALU.mult),
                 reads=[bfs, self.brs], writes=[bt])
            P.op("dve", lambda e: e.tensor_tensor(out=self.xT[:, dc, blk], in0=self.xT[:, dc, blk], in1=t[:],
                                                   op=ALU.add), reads=[bt, self.bx[tb]], writes=[self.bx[tb]])

    def alloc_ffn(self):
        P = self.P
        self.zT = P.sbuf("zT", [128, KC, TB], BF16)
        self.bz = Buf("z")
        self.hT = P.sbuf("hT", [128, FC, TB], BF16)
        self.bh = [Buf(f"h{i}") for i in range(FC)]
        self.fs = P.sbuf("fs", [128, KC, TB], BF16)
        self.bfs = Buf("fs")
        self.sa = [P.sbuf(f"sa{i}", [128, TB], F32) for i in range(2)]
        self.bsa = [Buf(f"sa{i}") for i in range(2)]

    def plan_ffn(self, w1, w3, w2):
        for tb in range(getattr(self, "lim_tb", NTB)):
            for fc in range(getattr(self, "lim_fc", FC)):
                self.ws.add(w1[fc], 2048)
                self.ws.add(w3[fc], 2048)
            for dc in range(getattr(self, "lim_dc", KC)):
                for (k0, n) in ((0, 16), (16, 16), (32, 12)):
                    self.ws.add(w2[dc, :, k0 * 128:(k0 + n) * 128], n * 128)

    def emit_ffn(self, l, f):
        P = self.P
        for tb in range(getattr(self, "lim_tb", NTB)):
            self.prenorm(l, 4 * f, tb, self.zT, self.bz)
            for fc in range(getattr(self, "lim_fc", FC)):
                w1, bw1 = self.ws.get()
                w3, bw3 = self.ws.get()
                pa, bpa = self.ps[fc % 2], self.bps[fc % 2]
                pb, bpb = self.ps[2 + fc % 2], self.bps[2 + fc % 2]
                for kc in range(KC):
                    P.op("pe", lambda e: e.matmul(pa[:], lhsT=w1[:, kc * 128:(kc + 1) * 128], rhs=self.zT[:, kc, :],
                                                  start=(kc == 0), stop=(kc == KC - 1)),
                         reads=[bw1, self.bz], writes=[bpa], inc=(kc == KC - 1))
                for kc in range(KC):
                    P.op("pe", lambda e: e.matmul(pb[:], lhsT=w3[:, kc * 128:(kc + 1) * 128], rhs=self.zT[:, kc, :],
                                                  start=(kc == 0), stop=(kc == KC - 1)),
                         reads=[bw3, self.bz], writes=[bpb], inc=(kc == KC - 1))
                sa, bsa = self.sa[fc % 2], self.bsa[fc % 2]
                P.op("act", lambda e: e.activation(out=sa[:], in_=pa[:], func=AF.Silu), reads=[bpa], writes=[bsa])
                P.op("dve", lambda e: e.tensor_tensor(out=self.hT[:, fc, :], in0=pb[:], in1=sa[:], op=ALU.mult),
                     reads=[bpb, bsa], writes=[self.bh[fc]])
            gi = (l * 6 + 4 * f + 1) * KC
            for dc in range(getattr(self, "lim_dc", KC)):
                pf, bpf = self.ps[4 + dc % 2], self.bps[4 + dc % 2]
                for (k0, n) in ((0, 16), (16, 16), (32, 12)):
                    w, bw = self.ws.get()
                    for j in range(n):
                        P.op("pe", lambda e: e.matmul(pf[:], lhsT=w[:, j * 128:(j + 1) * 128], rhs=self.hT[:, k0 + j, :],
                                                      start=(k0 + j == 0), stop=(k0 + j == FC - 1)),
                             reads=[bw, self.bh[k0 + j]], writes=[bpf], inc=(j == n - 1))
                import os
                if not os.environ.get("V1"):
                    P.op("dve", lambda e: e.tensor_scalar(out=self.fs[:, dc, :], in0=pf[:],
                                                       scalar1=self.gcol[:, gi + dc:gi + dc + 1], scalar2=0.5,
                                                       op0=ALU.mult, op1=ALU.mult), reads=[bpf, self.bg], writes=[self.bfs])
                if not os.environ.get("V2"):
                    self.stat_acc(pf[:], [bpf], dc == 0, dc == KC - 1)
            if getattr(self, "lim_dc", KC) == KC:
                self.postnorm_add(l, 4 * f + 1, tb, self.fs, self.bfs, self.sa, self.bsa)


SEQ = 2048
CH = 128
NCH = SEQ // CH
RW_GN_EPS = 64e-5
DECAY_C = float(np.exp(-0.5))


class RWB:
    def __init__(self, nc, stack):
        self.nc = nc
        self.P = Prog(nc, stack)

    def din(self, name, shape, dt=F32):
        return self.nc.dram_tensor(name, list(shape), dt, kind="ExternalInput").ap()

    def build(self):
        P = self.P
        nc = self.nc
        rkv = self.din("rkv", [3, 4, 128, SEQ])
        lora = self.din("lora", [128, SEQ])
        hga = self.din("hga", [128, SEQ])
        hgb = self.din("hgb", [32, SEQ])
        cols_d = self.din("cols", [128, 32])
        cols2_d = self.din("cols2", [128, 3])
        w2a2_d = self.din("w2a2", [128, 512])
        g2a_d = self.din("g2a", [128, 512])
        g2b_d = self.din("g2b", [32, 512])
        lng_d = self.din("lng", [128, 512])
        lnb_d = self.din("lnb", [128, 512])
        maskmt_d = self.din("mask_mt", [128, 256])
        maska_d = self.din("mask_a", [128, 128])
        ident_d = self.din("ident", [128, 128])
        blk_d = self.din("blkmask", [128, 128])
        resetm_d = self.din("resetm", [128, SEQ])
        yT = nc.dram_tensor("yT", [512, SEQ], BF16, kind="ExternalOutput").ap()

        def sb(name, shape, dt):
            return P.sbuf(name, shape, dt)

        psf = [P.psum(f"pb{i}", [128, 512]) for i in range(7)]
        pst = P.psum("pbt", [128, 1024], BF16)
        bk = [Buf(f"bank{i}") for i in range(8)]
        for b in bk:
            b.excl = True
        bT = bk[7]

        def load_const(name, shape, dt, src, q="pool"):
            t = sb(name + "_sb", shape, dt)
            b = Buf(name)
            P.dma(q, t[:], src, writes=[b], sem_buf=b)
            b.const = True
            return t, b

        cols, bcols = load_const("cols", [128, 32], F32, cols_d, "sp")
        cols2, bcols2 = load_const("cols2", [128, 3], F32, cols2_d, "sp")
        w2a2, bw2a2 = load_const("w2a2", [128, 512], BF16, w2a2_d)
        g2a, bg2a = load_const("g2a", [128, 512], BF16, g2a_d)
        g2b, bg2b = load_const("g2b", [32, 512], BF16, g2b_d)
        lng, blng = load_const("lng", [128, 512], F32, lng_d, "sp")
        lnb, blnb = load_const("lnb", [128, 512], F32, lnb_d, "sp")
        maskmt, bmaskmt = load_const("maskmt", [128, 256], BF16, maskmt_d)
        maska, bmaska = load_const("maska", [128, 128], BF16, maska_d)
        ident, bident = load_const("ident", [128, 128], BF16, ident_d)
        blkm, bblkm = load_const("blkm", [128, 128], F32, blk_d, "sp")
        resetm, bresetm = load_const("resetm", [128, SEQ], BF16, resetm_d)
        onesb = sb("onesb", [128, 1], BF16)
        bonesb = Buf("onesb")
        P.op("dve", lambda e: e.memset(onesb[:], 1.0), writes=[bonesb])
        bonesb.const = True
        consts = [bcols, bcols2]

        RAW = sb("RAW", [128, SEQ + 1], F32)
        bRAW = Buf("RAW")
        P.op("dve", lambda e: e.memset(RAW[:, 0:1], 0.0), writes=[bRAW])
        Tt = [sb(f"T{i}", [128, SEQ], F32) for i in range(6)]
        bTt = [Buf(f"T{i}") for i in range(6)]
        T1, T2, T3, T4, T5, T6 = Tt
        b1, b2, b3, b4, b5, b6 = bTt
        L = sb("L", [128, SEQ], BF16)
        bL = Buf("L")
        SGA = sb("SGA", [128, SEQ], BF16)
        bSGA = Buf("SGA")
        SGB = sb("SGB", [32, SEQ], BF16)
        bSGB = Buf("SGB")
        AR = sb("AR", [128, NCH, 2, CH], BF16)
        bAR = Buf("AR")
        KT = sb("KT", [128, SEQ], BF16)
        bKT = Buf("KT")
        BT = sb("BT", [128, SEQ], BF16)
        bBT = Buf("BT")
        VB = sb("VB", [128, SEQ], BF16)
        bVB = Buf("VB")
        PB = sb("PB", [128, SEQ], BF16)
        bPB = Buf("PB")
        KTM = sb("KTM", [128, NCH, 128], BF16)
        BTM = sb("BTM", [128, NCH, 128], BF16)
        VTM = sb("VTM", [128, NCH, 128], BF16)
        bTM = Buf("TM")
        gam = sb("gam", [128, NCH], F32)
        bgam = Buf("gam")
        bon = sb("bon", [128, 2 * NCH], F32)
        bbon = Buf("bon")
        MAK = sb("MAK", [128, 2 * NCH, 128], BF16)
        MRK = sb("MRK", [128, 2 * NCH, 128], BF16)
        MRB = sb("MRB", [128, 2 * NCH, 128], BF16)
        PF = sb("PF", [128, 2 * NCH, 128], BF16)
        bMAT = [Buf(f"mat{u}") for u in range(2 * NCH)]
        bPF = [Buf(f"pf{u}") for u in range(2 * NCH)]
        NB = 8
        CM = [sb(f"CM{i}", [128, NB, 128], BF16) for i in range(2)]
        CA = [sb(f"CA{i}", [128, NB, 128], BF16) for i in range(2)]
        CP = [sb(f"CP{i}", [128, NB, 128], BF16) for i in range(2)]
        bCM = [Buf(f"CM{i}") for i in range(2)]
        bCA = [Buf(f"CA{i}") for i in range(2)]
        bCP = [Buf(f"CP{i}") for i in range(2)]
        G2f = sb("G2f", [128, 128], F32)
        G2b = sb("G2b", [128, 128], BF16)
        bG = Buf("G")
        Gt = sb("Gt", [128, 128], F32)
        bGt = Buf("Gt")
        X2b = sb("X2b", [128, 128], BF16)
        bX2b = Buf("X2b")
        U2b = sb("U2b", [128, 128], BF16)
        bU2b = Buf("U2b")
        ytm = sb("ytm", [128, 128], F32)
        bytm = Buf("ytm")
        st6 = sb("st6", [128, 2, 6], F32)
        mv = sb("mv", [128, 2, 2], F32)
        bst = Buf("st")
        otm = sb("otm", [128, 128], BF16)
        botm = Buf("otm")
        YT = sb("YT", [128, SEQ], BF16)
        bYT = Buf("YT")

        TBK = 512

        def mix(dst, dstbufs, mucol, tmp, btmp):
            P.op("dve", lambda e: e.tensor_tensor(out=tmp[:], in0=RAW[:, 0:SEQ], in1=RAW[:, 1:SEQ + 1],
                                                   op=ALU.subtract), reads=[bRAW], writes=[btmp])
            P.op("dve", lambda e: e.scalar_tensor_tensor(out=dst, in0=tmp[:], scalar=mucol, in1=RAW[:, 1:SEQ + 1],
                                                          op0=ALU.mult, op1=ALU.add),
                 reads=[bRAW, btmp] + consts, writes=dstbufs)

        def mix_part(dst, dstbufs, mucol, tmp, btmp, rows):
            r = slice(rows[0], rows[1])
            P.op("dve", lambda e: e.tensor_tensor(out=tmp[r, :], in0=RAW[r, 0:SEQ], in1=RAW[r, 1:SEQ + 1],
                                                   op=ALU.subtract), reads=[bRAW], writes=[btmp])
            P.op("dve", lambda e: e.scalar_tensor_tensor(out=dst, in0=tmp[r, :], scalar=mucol, in1=RAW[r, 1:SEQ + 1],
                                                          op0=ALU.mult, op1=ALU.add),
                 reads=[bRAW, btmp] + consts, writes=dstbufs)

        P.dma("sp", RAW[:, 1:SEQ + 1], lora, writes=[bRAW], sem_buf=bRAW)
        mix(T1[:], [b1], cols2[:, 0:1], T2, b2)
        P.op("act", lambda e: e.activation(out=L[0:64, :], in_=T1[0:64, :], func=AF.Tanh), reads=[b1], writes=[bL])
        P.op("act", lambda e: e.activation(out=L[64:128, :], in_=T1[64:128, :], func=AF.Copy), reads=[b1], writes=[bL])
        P.dma("sp", RAW[:, 1:SEQ + 1], hga, writes=[bRAW], sem_buf=bRAW)
        mix(T1[:], [b1], cols2[:, 1:2], T2, b2)
        P.op("act", lambda e: e.activation(out=SGA[:], in_=T1[:], func=AF.Sigmoid), reads=[b1], writes=[bSGA])
        P.dma("sp", RAW[0:32, 1:SEQ + 1], hgb, writes=[bRAW], sem_buf=bRAW)
        mix_part(T1[0:32, :], [b1], cols2[0:32, 2:3], T2, b2, (0, 32))
        P.op("act", lambda e: e.activation(out=SGB[:], in_=T1[0:32, :], func=AF.Sigmoid), reads=[b1], writes=[bSGB])

        for hp in range(4):
            fcols = slice(hp * 128, (hp + 1) * 128)
            c_mur, c_muk, c_muv = cols[:, hp:hp + 1], cols[:, 4 + hp:5 + hp], cols[:, 8 + hp:9 + hp]
            c_w0, c_a0 = cols[:, 12 + hp:13 + hp], cols[:, 16 + hp:17 + hp]
            c_kk, c_ka, c_rk = cols[:, 20 + hp:21 + hp], cols[:, 24 + hp:25 + hp], cols[:, 28 + hp:29 + hp]
            P.dma("sp", RAW[:, 1:SEQ + 1], rkv[1, hp], writes=[bRAW], sem_buf=bRAW)
            mix(T1[:], [b1], c_muk, T1, b1)
            for tb in range(SEQ // TBK):
                ts_ = slice(tb * TBK, (tb + 1) * TBK)
                P.op("pe", lambda e: e.matmul(psf[0][:], lhsT=w2a2[64:128, fcols], rhs=L[64:128, ts_], start=True,
                                              stop=True), reads=[bw2a2, bL], writes=[bk[0]])
                P.op("act", lambda e: e.activation(out=T2[:, ts_], in_=psf[0][:], func=AF.Sigmoid, bias=c_a0),
                     reads=[bk[0]] + consts, writes=[b2])
            P.op("dve", lambda e: e.tensor_scalar(out=T3[:], in0=T1[:], scalar1=c_kk, scalar2=None, op0=ALU.mult),
                 reads=[b1] + consts, writes=[b3])
            P.op("act", lambda e: e.activation(out=T4[:], in_=T3[:], func=AF.Square), reads=[b3], writes=[b4])
            for tb in range(SEQ // TBK):
                ts_ = slice(tb * TBK, (tb + 1) * TBK)
                P.op("pe", lambda e: e.matmul(psf[1][:], lhsT=blkm[:], rhs=T4[:, ts_], start=True, stop=True),
                     reads=[bblkm, b4], writes=[bk[1]])
                P.op("act", lambda e: e.activation(out=T5[:, ts_], in_=psf[1][:], func=AF.Sqrt), reads=[bk[1]],
                     writes=[b5])
            P.op("dve", lambda e: e.tensor_scalar(out=T5[:], in0=T5[:], scalar1=1e-12, scalar2=None, op0=ALU.max),
                 reads=[b5], writes=[b5])
            P.op("dve", lambda e: e.reciprocal(out=T5[:], in_=T5[:]), reads=[b5], writes=[b5])
            P.op("dve", lambda e: e.tensor_tensor(out=T3[:], in0=T3[:], in1=T5[:], op=ALU.mult), reads=[b3, b5],
                 writes=[b3])
            P.op("dve", lambda e: e.tensor_scalar(out=T4[:], in0=T2[:], scalar1=c_ka, scalar2=c_ka, op0=ALU.mult,
                                                   op1=ALU.subtract), reads=[b2] + consts, writes=[b4])
            P.op("dve", lambda e: e.scalar_tensor_tensor(out=T1[:], in0=T4[:], scalar=1.0, in1=T1[:], op0=ALU.add,
                                                          op1=ALU.mult), reads=[b4, b1], writes=[b1])
            P.op("dve", lambda e: e.tensor_tensor(out=T2[:], in0=T3[:], in1=T2[:], op=ALU.mult), reads=[b3, b2],
                 writes=[b2])
            for tb in range(SEQ // TBK):
                ts_ = slice(tb * TBK, (tb + 1) * TBK)
                P.op("pe", lambda e: e.matmul(psf[0][:], lhsT=w2a2[0:64, fcols], rhs=L[0:64, ts_], start=True,
                                              stop=True), reads=[bw2a2, bL], writes=[bk[0]])
                P.op("act", lambda e: e.activation(out=T4[:, ts_], in_=psf[0][:], func=AF.Sigmoid, bias=c_w0),
                     reads=[bk[0]] + consts, writes=[b4])
            P.op("dve", lambda e: e.tensor_scalar(out=T4[:], in0=T4[:], scalar1=-DECAY_C, scalar2=None, op0=ALU.mult),
                 reads=[b4], writes=[b4])
            P.op("dve", lambda e: e.tensor_tensor_scan(out=T5[:], data0=resetm[:], data1=T4[:], initial=0.0,
                                                        op0=ALU.mult, op1=ALU.add), reads=[b4, bresetm], writes=[b5])
            P.op("dve", lambda e: e.tensor_tensor(out=T4[:], in0=T5[:], in1=T4[:], op=ALU.subtract), reads=[b4, b5],
                 writes=[b4])
            P.op("act", lambda e: e.activation(out=T4[:], in_=T4[:], func=AF.Exp), reads=[b4], writes=[b4])
            P.op("act", lambda e: e.activation(out=T6[:], in_=T5[:], func=AF.Exp), reads=[b5], writes=[b6])
            P.op("act", lambda e: e.activation(out=T5[:], in_=T5[:], func=AF.Exp, scale=-1.0), reads=[b5],
                 writes=[b5])
            P.op("dve", lambda e: e.tensor_copy(out=gam[:], in_=T6[:].rearrange("p (c t) -> p c t", t=CH)[:, :, CH - 1]),
                 reads=[b6], writes=[bgam])
            ARv = AR[:]
            P.op("dve", lambda e: e.scalar_tensor_tensor(out=ARv[:, :, 0, :], in0=T3[:].rearrange("p (c t) -> p c t", t=CH),
                                                          scalar=-1.0, in1=T4[:].rearrange("p (c t) -> p c t", t=CH),
                                                          op0=ALU.mult, op1=ALU.mult), reads=[b3, b4], writes=[bAR])
            P.op("dve", lambda e: e.tensor_tensor(out=KT[:], in0=T1[:], in1=T5[:], op=ALU.mult), reads=[b1, b5],
                 writes=[bKT])
            P.op("dve", lambda e: e.tensor_tensor(out=BT[:], in0=T2[:], in1=T5[:], op=ALU.mult), reads=[b2, b5],
                 writes=[bBT])
            if getattr(self, "stop", 99) == 1:
                P.finish()
                return
            P.dma("sp", RAW[:, 1:SEQ + 1], rkv[0, hp], writes=[bRAW], sem_buf=bRAW)
            mix(T3[:], [b3], c_mur, T3, b3)
            P.op("dve", lambda e: e.tensor_tensor(out=ARv[:, :, 1, :], in0=T3[:].rearrange("p (c t) -> p c t", t=CH),
                                                   in1=T6[:].rearrange("p (c t) -> p c t", t=CH), op=ALU.mult),
                 reads=[b3, b6], writes=[bAR])
            P.op("dve", lambda e: e.scalar_tensor_tensor(out=PB[:], in0=T3[:], scalar=c_rk, in1=T1[:], op0=ALU.mult,
                                                          op1=ALU.mult), reads=[b3, b1] + consts, writes=[bPB])
            P.dma("sp", RAW[:, 1:SEQ + 1], rkv[2, hp], writes=[bRAW], sem_buf=bRAW)
            mix(VB[:], [bVB], c_muv, T2, b2)
            if getattr(self, "stop", 99) == 2:
                P.finish()
                return
            for h in range(2):
                hr = slice(h * 64, (h + 1) * 64)
                for c in range(NCH):
                    P.op("pe", lambda e: e.matmul(psf[h][:, c:c + 1], lhsT=PB[hr, c * CH:(c + 1) * CH],
                                                  rhs=onesb[hr, :], start=True, stop=True),
                         reads=[bPB, bonesb], writes=[bk[h]], inc=(c == NCH - 1))
                P.op("dve", lambda e: e.tensor_copy(out=bon[:, h * NCH:(h + 1) * NCH], in_=psf[h][:, 0:NCH]),
                     reads=[bk[h]], writes=[bbon])
            if getattr(self, "stop", 99) == 3:
                P.finish()
                return
            for (src, bsrc, dst) in ((KT, bKT, KTM), (BT, bBT, BTM), (VB, bVB, VTM)):
                for g in range(NCH // 8):
                    for i in range(8):
                        c = g * 8 + i
                        P.op("pe", lambda e: e.transpose(pst[:, i * 128:(i + 1) * 128], src[:, c * CH:(c + 1) * CH],
                                                         ident[:]), reads=[bsrc, bident], writes=[bT], inc=(i == 7))
                    P.op("act", lambda e: e.activation(out=dst[:, g * 8:(g + 1) * 8, :],
                                                       in_=pst[:].rearrange("p (i f) -> p i f", f=128), func=AF.Copy),
                         reads=[bT], writes=[bTM])

            if getattr(self, "stop", 99) == 4:
                P.finish()
                return
            cb = [2, 3, 4, 5, 6]
            nrot = [0]

            def nextbank():
                i = cb[nrot[0] % len(cb)]
                nrot[0] += 1
                return i

            evq = [0]

            def evac_copy(dst_ap, src_ap, rb, wb):
                e = "act" if evq[0] % 2 == 0 else "dve"
                evq[0] += 1
                if e == "act":
                    P.op("act", lambda en: en.activation(out=dst_ap, in_=src_ap, func=AF.Copy), reads=rb, writes=wb)
                else:
                    P.op("dve", lambda en: en.tensor_copy(out=dst_ap, in_=src_ap), reads=rb, writes=wb)

            for bi in range(2 * NCH // NB):
                units = [bi * NB + i for i in range(NB)]
                for i, u in enumerate(units):
                    c, h = u // 2, u % 2
                    hr = slice(h * 64, (h + 1) * 64)
                    tok = slice(c * CH, (c + 1) * CH)
                    k_ = nextbank()
                    P.op("pe", lambda e: e.matmul(psf[k_][:, 0:256], lhsT=BT[hr, tok], rhs=AR[hr, c].rearrange("p a b -> p (a b)"), start=True,
                                                  stop=True), reads=[bBT, bAR], writes=[bk[k_]], inc=False)
                    P.op("pe", lambda e: e.matmul(psf[k_][:, 256:512], lhsT=KT[hr, tok], rhs=AR[hr, c].rearrange("p a b -> p (a b)"), start=True,
                                                  stop=True), reads=[bKT, bAR], writes=[bk[k_]])
                    P.op("dve", lambda e: e.tensor_tensor(out=CM[0][:, i, :], in0=psf[k_][:, 0:128], in1=maskmt[:, 0:128],
                                                           op=ALU.mult), reads=[bk[k_], bmaskmt], writes=[bCM[0]])
                    P.op("dve", lambda e: e.tensor_tensor(out=MRB[:, u, :], in0=psf[k_][:, 128:256],
                                                           in1=maskmt[:, 128:256], op=ALU.mult),
                         reads=[bk[k_], bmaskmt], writes=[bMAT[u]])
                    P.op("dve", lambda e: e.tensor_tensor(out=MAK[:, u, :], in0=psf[k_][:, 256:384], in1=maskmt[:, 0:128],
                                                           op=ALU.mult), reads=[bk[k_], bmaskmt], writes=[bMAT[u]])
                    P.op("dve", lambda e: e.tensor_tensor(out=MRK[:, u, :], in0=psf[k_][:, 384:512],
                                                           in1=maskmt[:, 128:256], op=ALU.mult),
                         reads=[bk[k_], bmaskmt], writes=[bMAT[u]])
                for g in range(2):
                    k_ = nextbank()
                    for i4 in range(NB // 2):
                        i = 2 * i4 + g
                        u = units[i]
                        c, h = u // 2, u % 2
                        hr = slice(h * 64, (h + 1) * 64)
                        tok = slice(c * CH, (c + 1) * CH)
                        P.op("pe", lambda e: e.matmul(psf[k_][:, i4 * 128:(i4 + 1) * 128], lhsT=ARv[hr, c, 0, :],
                                                      rhs=BT[hr, tok], start=True, stop=True), reads=[bAR, bBT],
                             writes=[bk[k_]], inc=(i4 == NB // 2 - 1))
                    for i4 in range(NB // 2):
                        i = 2 * i4 + g
                        P.op("dve", lambda e: e.tensor_tensor(out=CA[0][:, i, :], in0=psf[k_][:, i4 * 128:(i4 + 1) * 128],
                                                               in1=maska[:], op=ALU.mult), reads=[bk[k_], bmaska],
                             writes=[bCA[0]])
                if getattr(self, "stop", 99) == 5:
                    P.finish()
                    return
                for i in range(NB):
                    P.op("dve", lambda e: e.tensor_tensor(out=CP[0][:, i, :], in0=CM[0][:, i, :], in1=ident[:], op=ALU.add),
                         reads=[bCM[0], bident], writes=[bCP[0]])
                cur = 0
                for k in range(1, 7):
                    nxt = 1 - cur
                    last = (k == 6)
                    for g in range(NB // 4):
                        sl4 = slice(g * 4, (g + 1) * 4)
                        if not last:
                            k_ = nextbank()
                            for i4 in range(4):
                                i = g * 4 + i4
                                P.op("pe", lambda e: e.matmul(psf[k_][:, i4 * 128:(i4 + 1) * 128], lhsT=CA[cur][:, i, :],
                                                              rhs=CM[cur][:, i, :], start=True, stop=True),
                                     reads=[bCA[cur], bCM[cur]], writes=[bk[k_]], inc=(i4 == 3))
                            evac_copy(CM[nxt][:, sl4, :], psf[k_][:].rearrange("p (i f) -> p i f", f=128), [bk[k_]],
                                      [bCM[nxt]])
                        k_ = nextbank()
                        for i4 in range(4):
                            i = g * 4 + i4
                            P.op("pe", lambda e: e.matmul(psf[k_][:, i4 * 128:(i4 + 1) * 128], lhsT=CM[cur][:, i, :],
                                                          rhs=CA[cur][:, i, :], start=True, stop=True),
                                 reads=[bCA[cur], bCM[cur]], writes=[bk[k_]], inc=(i4 == 3))
                        evac_copy(CA[nxt][:, sl4, :], psf[k_][:].rearrange("p (i f) -> p i f", f=128), [bk[k_]],
                                  [bCA[nxt]])
                    for g in range(NB // 4):
                        sl4 = slice(g * 4, (g + 1) * 4)
                        k_ = nextbank()
                        for i4 in range(4):
                            i = g * 4 + i4
                            P.op("pe", lambda e: e.matmul(psf[k_][:, i4 * 128:(i4 + 1) * 128], lhsT=ident[:],
                                                          rhs=CP[cur][:, i, :], start=True, stop=False),
                                 reads=[bident, bCP[cur]], writes=[bk[k_]], inc=False)
                            P.op("pe", lambda e: e.matmul(psf[k_][:, i4 * 128:(i4 + 1) * 128], lhsT=CA[nxt][:, i, :],
                                                          rhs=CP[cur][:, i, :], start=False, stop=True),
                                 reads=[bCA[nxt], bCP[cur]], writes=[bk[k_]], inc=(i4 == 3))
                        if last:
                            u0 = units[g * 4]
                            evac_copy(PF[:, u0:u0 + 4, :], psf[k_][:].rearrange("p (i f) -> p i f", f=128), [bk[k_]],
                                      [bPF[u] for u in units[g * 4:g * 4 + 4]])
                        else:
                            evac_copy(CP[nxt][:, sl4, :], psf[k_][:].rearrange("p (i f) -> p i f", f=128), [bk[k_]],
                                      [bCP[nxt]])
                    cur = nxt

            if getattr(self, "stop", 99) == 6:
                P.finish()
                return
            P.op("dve", lambda e: e.memset(G2f[:], 0.0), writes=[bG])
            P.op("dve", lambda e: e.memset(G2b[:], 0.0), writes=[bG])
            bX, bY, bGn = 2, 3, 4
            for c in range(NCH):
                tok = slice(c * CH, (c + 1) * CH)
                P.op("pe", lambda e: e.matmul(psf[bX][:, 0:128], lhsT=ARv[:, c, 0, :], rhs=G2b[:], start=True, stop=False),
                     reads=[bAR, bG], writes=[bk[bX]], inc=False)
                for h in range(2):
                    hc = slice(h * 64, (h + 1) * 64)
                    P.op("pe", lambda e: e.matmul(psf[bX][:, hc], lhsT=MAK[:, 2 * c + h, :], rhs=VTM[:, c, hc],
                                                  start=False, stop=(h == 1)), reads=[bMAT[2 * c + h], bTM],
                         writes=[bk[bX]], inc=(h == 1))
                P.op("act", lambda e: e.activation(out=X2b[:], in_=psf[bX][:, 0:128], func=AF.Copy), reads=[bk[bX]],
                     writes=[bX2b])
                for h in range(2):
                    hc = slice(h * 64, (h + 1) * 64)
                    P.op("pe", lambda e: e.matmul(psf[bX][:, 128 + h * 64:128 + (h + 1) * 64], lhsT=PF[:, 2 * c + h, :],
                                                  rhs=X2b[:, hc], start=True, stop=True), reads=[bPF[2 * c + h], bX2b],
                         writes=[bk[bX]], inc=(h == 1))
                P.op("dve", lambda e: e.tensor_copy(out=U2b[:], in_=psf[bX][:, 128:256]), reads=[bk[bX]], writes=[bU2b])
                P.op("pe", lambda e: e.matmul(psf[bY][:, 0:128], lhsT=ARv[:, c, 1, :], rhs=G2b[:], start=True, stop=False),
                     reads=[bAR, bG], writes=[bk[bY]], inc=False)
                for h in range(2):
                    hc = slice(h * 64, (h + 1) * 64)
                    P.op("pe", lambda e: e.matmul(psf[bY][:, hc], lhsT=MRB[:, 2 * c + h, :], rhs=U2b[:, hc], start=False,
                                                  stop=False), reads=[bMAT[2 * c + h], bU2b], writes=[bk[bY]], inc=False)
                    P.op("pe", lambda e: e.matmul(psf[bY][:, hc], lhsT=MRK[:, 2 * c + h, :], rhs=VTM[:, c, hc], start=False,
                                                  stop=(h == 1)), reads=[bMAT[2 * c + h], bTM], writes=[bk[bY]],
                         inc=(h == 1))
                P.op("pe", lambda e: e.matmul(psf[bGn][:, 0:128], lhsT=BTM[:, c, :], rhs=U2b[:], start=True, stop=False),
                     reads=[bTM, bU2b], writes=[bk[bGn]], inc=False)
                P.op("pe", lambda e: e.matmul(psf[bGn][:, 0:128], lhsT=KTM[:, c, :], rhs=VTM[:, c, :], start=False,
                                              stop=True), reads=[bTM], writes=[bk[bGn]])
                P.op("dve", lambda e: e.tensor_tensor(out=Gt[:], in0=psf[bGn][:, 0:128], in1=G2f[:], op=ALU.add),
                     reads=[bk[bGn], bG], writes=[bGt])
                P.op("dve", lambda e: e.scalar_tensor_tensor(out=G2f[:], in0=Gt[:], scalar=gam[:, c:c + 1], in1=blkm[:],
                                                              op0=ALU.mult, op1=ALU.mult), reads=[bGt, bgam, bblkm],
                     writes=[bG])
                P.op("act", lambda e: e.activation(out=G2b[:], in_=G2f[:], func=AF.Copy), reads=[bG], writes=[bG])
                if getattr(self, "stop", 99) == 7:
                    P.finish()
                    return
                P.op("act", lambda e: e.activation(out=ytm[:], in_=psf[bY][:, 0:128], func=AF.Copy), reads=[bk[bY]],
                     writes=[bytm])
                for h in range(2):
                    hc = slice(h * 64, (h + 1) * 64)
                    P.op("dve", lambda e: e.bn_stats(out=st6[:, h, :], in_=ytm[:, hc]), reads=[bytm], writes=[bst])
                    P.op("dve", lambda e: e.bn_aggr(out=mv[:, h, :], in_=st6[:, h, :]), reads=[bst], writes=[bst])
                P.op("dve", lambda e: e.tensor_scalar(out=mv[:, :, 1], in0=mv[:, :, 1], scalar1=RW_GN_EPS, scalar2=None,
                                                       op0=ALU.add), reads=[bst], writes=[bst])
                P.op("act", lambda e: e.activation(out=mv[:, :, 1], in_=mv[:, :, 1], func=AF.Sqrt), reads=[bst],
                     writes=[bst])
                P.op("dve", lambda e: e.reciprocal(out=mv[:, :, 1], in_=mv[:, :, 1]), reads=[bst], writes=[bst])
                for h in range(2):
                    hc = slice(h * 64, (h + 1) * 64)
                    P.op("dve", lambda e: e.tensor_scalar(out=ytm[:, hc], in0=ytm[:, hc], scalar1=mv[:, h, 0:1],
                                                           scalar2=mv[:, h, 1:2], op0=ALU.subtract, op1=ALU.mult),
                         reads=[bytm, bst], writes=[bytm])
                P.op("dve", lambda e: e.tensor_tensor(out=ytm[:], in0=ytm[:], in1=lng[:, fcols], op=ALU.mult),
                     reads=[bytm, blng], writes=[bytm])
                P.op("dve", lambda e: e.tensor_tensor(out=ytm[:], in0=ytm[:], in1=lnb[:, fcols], op=ALU.add),
                     reads=[bytm, blnb], writes=[bytm])
                for h in range(2):
                    hc = slice(h * 64, (h + 1) * 64)
                    P.op("dve", lambda e: e.scalar_tensor_tensor(out=ytm[:, hc], in0=VTM[:, c, hc],
                                                                  scalar=bon[:, h * NCH + c:h * NCH + c + 1], in1=ytm[:, hc],
                                                                  op0=ALU.mult, op1=ALU.add), reads=[bTM, bbon, bytm],
                         writes=[bytm])
                if getattr(self, "stop", 99) == 8:
                    P.finish()
                    return
                P.op("pe", lambda e: e.matmul(psf[bY][:, 128:256], lhsT=SGA[:, tok], rhs=g2a[:, fcols], start=True,
                                              stop=False), reads=[bSGA, bg2a], writes=[bk[bY]], inc=False)
                P.op("pe", lambda e: e.matmul(psf[bY][:, 128:256], lhsT=SGB[0:32, tok], rhs=g2b[0:32, fcols], start=False,
                                              stop=True), reads=[bSGB, bg2b], writes=[bk[bY]])
                P.op("dve", lambda e: e.tensor_tensor(out=otm[:], in0=psf[bY][:, 128:256], in1=ytm[:], op=ALU.mult),
                     reads=[bk[bY], bytm], writes=[botm])
                P.op("pe", lambda e: e.transpose(pst[:, 0:128], otm[:], ident[:]), reads=[botm, bident], writes=[bT])
                P.op("act", lambda e: e.activation(out=YT[:, tok], in_=pst[:, 0:128], func=AF.Copy), reads=[bT],
                     writes=[bYT])
            P.dma("sp", yT[hp * 128:(hp + 1) * 128, :], YT[:], reads=[bYT], sem_buf=bYT, kind="st", is_output=True)
        P.finish()


def rw_consts():
    p = np.arange(128)[:, None]
    q = np.arange(128)[None, :]
    mask_mt = np.concatenate([(q > p), (q >= p)], axis=1).astype(np.float32)
    mask_a = (q < p).astype(np.float32)
    ident = np.eye(128, dtype=np.float32)
    blk = ((p // 64) == (q // 64)).astype(np.float32)
    resetm = np.ones((128, SEQ), np.float32)
    resetm[:, ::CH] = 0.0
    return {"mask_mt": mask_mt, "mask_a": mask_a, "ident": ident, "blkmask": blk, "resetm": resetm}


def rw_core_inputs(pT_b, e, j, prm):
    f0 = j * 512
    def rows(base):
        return pT_b[base + f0: base + f0 + 512].reshape(4, 128, SEQ)
    rkv = np.stack([rows(8 * 128), rows(16 * 128), rows(24 * 128)], axis=0)
    lora = pT_b[32 * 128:33 * 128]
    hga = pT_b[33 * 128:34 * 128]
    hgb = pT_b[34 * 128:34 * 128 + 32]
    mu = prm["ev_mu"][e]
    RD = 1024
    def colsof(v):
        return v[f0:f0 + 512].reshape(4, 128).T
    cols = np.concatenate([
        colsof(mu[0:RD]), colsof(mu[RD:2 * RD]), colsof(mu[2 * RD:3 * RD]),
        colsof(prm["rw_w0"][e]), colsof(prm["rw_a0"][e]), colsof(prm["rw_kk"][e]), colsof(prm["rw_ka"][e]),
        colsof(prm["rw_rk"][e].reshape(-1))], axis=1)
    cols2 = np.zeros((128, 3), np.float32)
    cols2[:, 0] = mu[3 * RD:3 * RD + 128]
    cols2[:, 1] = mu[3 * RD + 128:3 * RD + 256]
    cols2[0:32, 2] = mu[3 * RD + 256:3 * RD + 288]
    w2a2 = np.concatenate([prm["rw_w2"][e][:, f0:f0 + 512], prm["rw_a2"][e][:, f0:f0 + 512]], axis=0)
    g2 = prm["rw_g2"][e][:, f0:f0 + 512]
    d = {"rkv": rkv, "lora": lora, "hga": hga, "hgb": hgb, "cols": cols, "cols2": cols2, "w2a2": w2a2,
         "g2a": g2[0:128], "g2b": g2[128:160],
         "lng": np.broadcast_to(prm["rw_ln_g"][e][f0:f0 + 512], (128, 512)),
         "lnb": np.broadcast_to(prm["rw_ln_b"][e][f0:f0 + 512], (128, 512))}
    d.update(rw_consts())
    return {k: np.ascontiguousarray(v, dtype=np.float32) for k, v in d.items()}


LN_EPS = 1e-5
CVW = 31
HALO = 32


def _mixer_alloc(self):
    P = self.P
    f32v = self.hT[:].rearrange("p a b -> p (a b)").bitcast(F32)
    o = 8 * TB // 2
    self.cv = f32v[:, o:o + 8 * TB].rearrange("p (c t) -> p c t", t=TB)
    self.bcv = [Buf(f"cv{i}") for i in range(8)]
    o += 8 * TB
    self.stg = [f32v[:, o + i * TB:o + (i + 1) * TB] for i in range(4)]
    self.bstg = [Buf(f"stg{i}") for i in range(4)]
    self.nstg = 0
    o += 4 * TB
    W = HALO + TB
    self.wk = [f32v[:, o + i * W:o + (i + 1) * W] for i in range(5)]
    self.bwk = [Buf(f"wk{i}") for i in range(5)]
    assert o + 5 * W <= FC * TB // 2
    self.mean = P.sbuf("mean", [128, TB], F32)
    self.bmean = Buf("mean")
    self.identb = P.sbuf("identb", [128, 128], BF16)
    self.bidentb = Buf("identb")
    self.maskup = P.sbuf("maskup", [128, 128], BF16)
    self.bmaskup = Buf("maskup")
    self.pcols = P.sbuf("pcols", [128, 64], F32)
    self.bpcols = Buf("pcols")
    self.cvw = P.sbuf("cvw", [128, 8, CVW], F32)
    self.bcvw = Buf("cvw")
    self.wsT = P.sbuf("wsT", [128, 8, 128], BF16)
    self.bwsT = Buf("wsT")
    self.sgb = P.sbuf("sgb", [128, 8, 128], F32)
    self.bsgb = Buf("sgb")
    self.cfix = P.sbuf("cfix", [128, 4, 16], F32)
    self.bcfix = Buf("cfix")
    self.vn = P.sbuf("vn", [128, TB], BF16)
    self.bvn = Buf("vn")
    self.vtm = P.sbuf("vtm", [128, 4, 128], BF16)
    self.bvtm = Buf("vtm")


def _next_stg(self):
    i = self.nstg % 4
    self.nstg += 1
    return self.stg[i], self.bstg[i]


def _plan_inproj(self, w_tiled, nchunks, order=None):
    order = list(range(nchunks)) if order is None else order
    for tb in range(NTB):
        for c in order:
            self.ws.add(w_tiled[c], 2048)


def _emit_inproj(self, l, nchunks, order, evac):
    P = self.P
    for tb in range(NTB):
        self.prenorm(l, 2, tb, self.zT, self.bz)
        for n, c in enumerate(order):
            w, bw = self.ws.get()
            ps, bp = self.ps[n % 4], self.bps[n % 4]
            for kc in range(KC):
                P.op("pe", lambda e: e.matmul(ps[:], lhsT=w[:, kc * 128:(kc + 1) * 128], rhs=self.zT[:, kc, :],
                                              start=(kc == 0), stop=(kc == KC - 1)), reads=[bw, self.bz], writes=[bp],
                     inc=(kc == KC - 1))
            evac(tb, c, ps, bp)


def _emit_inproj_even(self, l, pT_out, bpT):
    P = self.P

    def evac(tb, c, ps, bp):
        st, bs = self._next_stg()
        if c % 2 == 0:
            P.op("act", lambda e: e.activation(out=st[:], in_=ps[:], func=AF.Copy), reads=[bp], writes=[bs])
        else:
            P.op("dve", lambda e: e.tensor_copy(out=st[:], in_=ps[:]), reads=[bp], writes=[bs])
        P.dma("sp", pT_out[c * 128:(c + 1) * 128, tb * TB:(tb + 1) * TB], st[:], reads=[bs], writes=[bpT], sem_buf=bs,
              kind="st", is_output=True)

    self._emit_inproj(l, 35, list(range(35)), evac)


ODD_ORDER = [x for c in range(8) for x in (c, c + 8)] + list(range(16, 32))


def _emit_inproj_odd(self, l, ycv_out, guv_out, bout):
    P = self.P
    hold = {}

    def evac(tb, c, ps, bp):
        if c < 8:
            st, bs = self._next_stg()
            P.op("dve", lambda e: e.tensor_copy(out=st[:], in_=ps[:]), reads=[bp], writes=[bs])
            hold[c] = (st, bs)
        elif c < 16:
            a, ba = hold.pop(c - 8)
            st, bs = self._next_stg()
            P.op("act", lambda e: e.activation(out=st[:], in_=ps[:], func=AF.Sigmoid), reads=[bp], writes=[bs])
            P.op("dve", lambda e: e.tensor_tensor(out=st[:], in0=st[:], in1=a[:], op=ALU.mult), reads=[bs, ba],
                 writes=[bs])
            P.dma("sp", ycv_out[(c - 8) * 128:(c - 7) * 128, tb * TB:(tb + 1) * TB], st[:], reads=[bs], writes=[bout],
                  sem_buf=bs, kind="st", is_output=True)
        else:
            st, bs = self._next_stg()
            P.op("act", lambda e: e.activation(out=st[:], in_=ps[:], func=AF.Gelu), reads=[bp], writes=[bs])
            P.dma("sp", guv_out[(c - 16) * 128:(c - 15) * 128, tb * TB:(tb + 1) * TB], st[:], reads=[bs], writes=[bout],
                  sem_buf=bs, kind="st", is_output=True)

    self._emit_inproj(l, 32, ODD_ORDER, evac)


def _plan_outproj(self, wout_tiled, pool_tiled=None):
    for tb in range(NTB):
        if pool_tiled is not None:
            for g in range(4):
                for dc in range(2):
                    self.ws.add(pool_tiled[g, dc], 256)
        for dc in range(KC):
            self.ws.add(wout_tiled[dc], 2048)


def _emit_outproj(self, l, tb):
    P = self.P
    gi = (l * 6 + 3) * KC
    for dc in range(KC):
        pf, bpf = self.ps[4 + dc % 2], self.bps[4 + dc % 2]
        w, bw = self.ws.get()
        for kc in range(KC):
            P.op("pe", lambda e: e.matmul(pf[:], lhsT=w[:, kc * 128:(kc + 1) * 128], rhs=self.zT[:, kc, :],
                                          start=(kc == 0), stop=(kc == KC - 1)), reads=[bw, self.bz], writes=[bpf],
                 inc=(kc == KC - 1))
        P.op("dve", lambda e: e.tensor_scalar(out=self.fs[:, dc, :], in0=pf[:], scalar1=self.gcol[:, gi + dc:gi + dc + 1],
                                               scalar2=None, op0=ALU.mult), reads=[bpf, self.bg], writes=[self.bfs])
        self.stat_acc(pf[:], [bpf], dc == 0, dc == KC - 1)
    self.postnorm_add(l, 3, tb, self.fs, self.bfs, self.sa, self.bsa)


def _load_layer_params(self, d):
    P = self.P
    if "pcols" in d:
        P.dma("sp", self.pcols[:], d["pcols"], writes=[self.bpcols], sem_buf=self.bpcols)
    if "cvw" in d:
        P.dma("sp", self.cvw[:], d["cvw"], writes=[self.bcvw], sem_buf=self.bcvw)
    if "wsT" in d:
        P.dma("pool", self.wsT[:], d["wsT"], writes=[self.bwsT], sem_buf=self.bwsT)
        for g in range(8):
            P.op("dve", lambda e: e.tensor_tensor(out=self.wsT[:, g, :], in0=self.wsT[:, g, :], in1=self.maskup[:],
                                                   op=ALU.mult), reads=[self.bwsT, self.bmaskup], writes=[self.bwsT])
    if "sgb" in d:
        P.dma("sp", self.sgb[:], d["sgb"], writes=[self.bsgb], sem_buf=self.bsgb)
    if "cfix" in d:
        P.dma("sp", self.cfix[:], d["cfix"], writes=[self.bcfix], sem_buf=self.bcfix)


def _load_mixer_consts(self, ident_d, maskup_d):
    P = self.P
    P.dma("pool", self.identb[:], ident_d, writes=[self.bidentb], sem_buf=self.bidentb)
    P.dma("pool", self.maskup[:], maskup_d, writes=[self.bmaskup], sem_buf=self.bmaskup)


def _emit_even_tail(self, l, ppool, phalo, yrw):
    P = self.P
    for tb in range(NTB):
        t0 = tb * TB
        P.dma("act", self.zT[:, 8:16, :], yrw[:, t0:t0 + TB].rearrange("(c p) t -> p c t", p=128), writes=[self.bz],
              sem_buf=self.bz)
        for c in range(8):
            g = c // 2
            x, bx_ = self.wk[c % 2], self.bwk[c % 2]
            if tb == 0:
                P.dma("sp", x[:, 0:HALO], phalo[c * 128:(c + 1) * 128, :], writes=[bx_], sem_buf=bx_)
                P.dma("sp", x[:, HALO:], ppool[c * 128:(c + 1) * 128, 0:TB], writes=[bx_], sem_buf=bx_)
            else:
                P.dma("sp", x[:], ppool[c * 128:(c + 1) * 128, t0 - HALO:t0 + TB], writes=[bx_], sem_buf=bx_)
            src, bsrc = x, bx_
            sh = 1
            for k in range(g + 1):
                dst, bdst = self.wk[2 + (k % 2)], self.bwk[2 + (k % 2)]
                P.op("dve", lambda e: e.tensor_tensor(out=dst[:, 16:], in0=src[:, 16:], in1=src[:, 16 - sh:HALO + TB - sh],
                                                       op=ALU.add), reads=[bsrc], writes=[bdst])
                src, bsrc = dst, bdst
                sh *= 2
            w = 2 ** (g + 1)
            d, bd = self.wk[4], self.bwk[4]
            P.op("dve", lambda e: e.scalar_tensor_tensor(out=d[:, HALO:], in0=src[:, HALO:], scalar=1.0 / w, in1=x[:, HALO:],
                                                          op0=ALU.mult, op1=ALU.subtract), reads=[bsrc, bx_], writes=[bd])
            if tb == 0:
                P.op("dve", lambda e: e.tensor_tensor(out=d[:, HALO:HALO + 16], in0=src[:, HALO:HALO + 16],
                                                       in1=self.cfix[:, g, :], op=ALU.mult), reads=[bsrc, self.bcfix],
                     writes=[bd])
                P.op("dve", lambda e: e.tensor_tensor(out=d[:, HALO:HALO + 16], in0=d[:, HALO:HALO + 16],
                                                       in1=x[:, HALO:HALO + 16], op=ALU.subtract), reads=[bx_, bd],
                     writes=[bd])
            P.op("act", lambda e: e.activation(out=self.hT[:, c, :], in_=d[:, HALO:], func=AF.Copy), reads=[bd],
                 writes=[self.bh[c]])
        for g in range(4):
            for dc in range(2):
                w_, bw = self.ws.get()
                ps, bp = self.ps[dc], self.bps[dc]
                for kc in range(2):
                    P.op("pe", lambda e: e.matmul(ps[:], lhsT=w_[:, kc * 128:(kc + 1) * 128], rhs=self.hT[:, 2 * g + kc, :],
                                                  start=(kc == 0), stop=(kc == 1)), reads=[bw, self.bh[2 * g + kc]],
                         writes=[bp], inc=(kc == 1))
                oc = 2 * g + dc
                P.op("dve", lambda e: e.tensor_scalar(out=self.zT[:, oc, :], in0=ps[:], scalar1=self.pcols[:, oc:oc + 1],
                                                       scalar2=None, op0=ALU.mult), reads=[bp, self.bpcols],
                     writes=[self.bz])
        self._emit_outproj(l, tb)


def _ln_stats(self, src_aps, src_bufs, n_feat):
    P = self.P
    n = len(src_aps)
    for i, ap in enumerate(src_aps):
        P.op("pe", lambda e: e.matmul(self.ps[5][:], lhsT=self.ones_f[:], rhs=ap, start=(i == 0), stop=(i == n - 1)),
             reads=src_bufs + [self.bones], writes=[self.bps[5]], inc=(i == n - 1))
    for i, ap in enumerate(src_aps):
        self.stat_acc(ap, src_bufs, i == 0, i == n - 1)
    P.op("act", lambda e: e.activation(out=self.mean[:], in_=self.ps[5][:], func=AF.Copy, scale=1.0 / n_feat),
         reads=[self.bps[5]], writes=[self.bmean])
    t, bt = self.sq[0], self.bsq[0]
    P.op("dve", lambda e: e.tensor_tensor(out=t[:], in0=self.mean[:], in1=self.mean[:], op=ALU.mult), reads=[self.bmean],
         writes=[bt])
    P.op("dve", lambda e: e.scalar_tensor_tensor(out=self.rs[:], in0=self.ps[6][:], scalar=1.0 / n_feat, in1=t[:],
                                                  op0=ALU.mult, op1=ALU.subtract), reads=[self.bps[6], bt],
         writes=[self.brs])
    P.op("dve", lambda e: e.tensor_scalar(out=self.rs[:], in0=self.rs[:], scalar1=LN_EPS, scalar2=None, op0=ALU.add),
         reads=[self.brs], writes=[self.brs])
    P.op("act", lambda e: e.activation(out=self.rs[:], in_=self.rs[:], func=AF.Sqrt), reads=[self.brs], writes=[self.brs])
    P.op("dve", lambda e: e.reciprocal(out=self.rs[:], in_=self.rs[:]), reads=[self.brs], writes=[self.brs])


def _emit_odd_tail(self, l, ycv, yhalo, guv):
    P = self.P
    for tb in range(NTB):
        t0 = tb * TB
        for c in range(8):
            x, bx_ = self.wk[c % 2], self.bwk[c % 2]
            if tb == 0:
                P.dma("sp", x[:, 0:HALO], yhalo[c * 128:(c + 1) * 128, :], writes=[bx_], sem_buf=bx_)
                P.dma("sp", x[:, HALO:], ycv[c * 128:(c + 1) * 128, 0:TB], writes=[bx_], sem_buf=bx_)
            else:
                P.dma("sp", x[:], ycv[c * 128:(c + 1) * 128, t0 - HALO:t0 + TB], writes=[bx_], sem_buf=bx_)
            acc = self.cv[:, c, :]
            bacc = self.bcv[c]
            for j in range(CVW):
                off = HALO - (CVW - 1) + j
                if j == 0:
                    P.op("dve", lambda e: e.tensor_scalar(out=acc, in0=x[:, off:off + TB], scalar1=self.cvw[:, c, 0:1],
                                                           scalar2=self.pcols[:, 16 + c:17 + c], op0=ALU.mult, op1=ALU.add),
                         reads=[bx_, self.bcvw, self.bpcols], writes=[bacc])
                else:
                    P.op("dve", lambda e: e.scalar_tensor_tensor(out=acc, in0=x[:, off:off + TB],
                                                                  scalar=self.cvw[:, c, j:j + 1], in1=acc, op0=ALU.mult,
                                                                  op1=ALU.add), reads=[bx_, self.bcvw, bacc], writes=[bacc])
        self._ln_stats([self.cv[:, c, :] for c in range(8)], self.bcv, 1024)
        for c in range(8):
            t, bt = self.wk[2 + c % 2], self.bwk[2 + c % 2]
            P.op("dve", lambda e: e.tensor_tensor(out=t[:, 0:TB], in0=self.cv[:, c, :], in1=self.mean[:], op=ALU.subtract),
                 reads=[self.bcv[c], self.bmean], writes=[bt])
            P.op("dve", lambda e: e.tensor_tensor(out=t[:, 0:TB], in0=t[:, 0:TB], in1=self.rs[:], op=ALU.mult),
                 reads=[bt, self.brs], writes=[bt])
            P.op("act", lambda e: e.activation(out=self.zT[:, c, :], in_=t[:, 0:TB], func=AF.Silu,
                                               scale=self.pcols[:, 24 + c:25 + c], bias=self.pcols[:, 32 + c:33 + c]),
                 reads=[bt, self.bpcols], writes=[self.bz])
        for c in range(8):
            P.dma("act", self.cv[:, c, :], guv[(8 + c) * 128:(9 + c) * 128, t0:t0 + TB], writes=[self.bcv[c]],
                  sem_buf=self.bcv[c])
        self._ln_stats([self.cv[:, c, :] for c in range(8)], self.bcv, 1024)
        for g in range(8):
            t, bt = self.wk[2 + g % 2], self.bwk[2 + g % 2]
            u, bu = self.wk[g % 2], self.bwk[g % 2]
            P.dma("sp", u[:, 0:TB], guv[g * 128:(g + 1) * 128, t0:t0 + TB], writes=[bu], sem_buf=bu)
            P.op("dve", lambda e: e.tensor_tensor(out=t[:, 0:TB], in0=self.cv[:, g, :], in1=self.mean[:], op=ALU.subtract),
                 reads=[self.bcv[g], self.bmean], writes=[bt])
            P.op("dve", lambda e: e.tensor_tensor(out=t[:, 0:TB], in0=t[:, 0:TB], in1=self.rs[:], op=ALU.mult),
                 reads=[bt, self.brs], writes=[bt])
            P.op("act", lambda e: e.activation(out=self.vn[:], in_=t[:, 0:TB], func=AF.Identity,
                                               scale=self.pcols[:, 40 + g:41 + g], bias=self.pcols[:, 48 + g:49 + g]),
                 reads=[bt, self.bpcols], writes=[self.bvn])
            for i in range(4):
                P.op("pe", lambda e: e.transpose(self.pstb[:, i * 128:(i + 1) * 128], self.vn[:, i * 128:(i + 1) * 128],
                                                 self.identb[:]), reads=[self.bvn, self.bidentb], writes=[self.bpst],
                     inc=(i == 3))
            P.op("act", lambda e: e.activation(out=self.vtm[:], in_=self.pstb[:, 0:512].rearrange("p (i f) -> p i f", f=128),
                                               func=AF.Copy), reads=[self.bpst], writes=[self.bvtm])
            ps, bp = self.ps[g % 2], self.bps[g % 2]
            for i in range(4):
                P.op("pe", lambda e: e.matmul(ps[:, i * 128:(i + 1) * 128], lhsT=self.vtm[:, i, :], rhs=self.wsT[:, g, :],
                                              start=True, stop=True), reads=[self.bvtm, self.bwsT], writes=[bp],
                     inc=(i == 3))
            s, bs_ = self.wk[4], self.bwk[4]
            P.op("dve", lambda e: e.tensor_tensor(out=s[:, 0:TB].rearrange("p (i f) -> p i f", f=128),
                                                   in0=ps[:].rearrange("p (i f) -> p i f", f=128),
                                                   in1=self.sgb[:, g, :].unsqueeze(1).to_broadcast([128, 4, 128]),
                                                   op=ALU.add), reads=[bp, self.bsgb], writes=[bs_])
            P.op("dve", lambda e: e.tensor_tensor(out=self.zT[:, 8 + g, :], in0=s[:, 0:TB], in1=u[:, 0:TB], op=ALU.mult),
                 reads=[bs_, bu], writes=[self.bz])
        self._emit_outproj(l, tb)


KB.mixer_alloc = _mixer_alloc
KB._next_stg = _next_stg
KB.plan_inproj = _plan_inproj
KB._emit_inproj = _emit_inproj
KB.emit_inproj_even = _emit_inproj_even
KB.emit_inproj_odd = _emit_inproj_odd
KB.plan_outproj = _plan_outproj
KB._emit_outproj = _emit_outproj
KB.load_layer_params = _load_layer_params
KB.load_mixer_consts = _load_mixer_consts
KB.emit_even_tail = _emit_even_tail
KB._ln_stats = _ln_stats
KB.emit_odd_tail = _emit_odd_tail


from contextlib import ExitStack

_PROG_CACHE = {}


def build_tok_launch(stages):
    key = ("tok", tuple(stages))
    if key in _PROG_CACHE:
        return _PROG_CACHE[key]
    nc = bass.Bass("TRN2", target_bir_lowering=False)
    with ExitStack() as st:
        kb = KB(nc, st)
        x_in = kb.din("x_in", [D, TOK])
        g_in = kb.din("gcol", [128, 2 * 6 * KC])
        xs = kb.dout("xs", [D, TOK])
        kinds = [s[0] for s in stages]
        has_tail = any(k in ("even_tail", "odd_tail") for k in kinds)
        has_mix = has_tail or any(k.startswith("inproj") for k in kinds)
        dr = {}
        for i, s in enumerate(stages):
            if s[0] == "ffn":
                dr[i] = (kb.din(f"w1_{i}", [FC, 128, D]), kb.din(f"w3_{i}", [FC, 128, D]), kb.din(f"w2_{i}", [KC, 128, DFF]))
            elif s[0] == "inproj_even":
                dr[i] = (kb.din("win", [35, 128, D]), kb.dout("pT", [35 * 128, TOK]))
            elif s[0] == "inproj_odd":
                dr[i] = (kb.din("win", [32, 128, D]), kb.dout("ycv_o", [1024, TOK]), kb.dout("guv_o", [2048, TOK]))
            elif s[0] == "even_tail":
                dr[i] = dict(wout=kb.din("wout", [KC, 128, D]), poolw=kb.din("poolw", [4, 2, 128, 256]),
                             ppool=kb.din("ppool", [1024, TOK]), phalo=kb.din("phalo", [1024, HALO]),
                             yrw=kb.din("yrw", [1024, TOK], BF16), pcols=kb.din("pcols", [128, 64]),
                             cfix=kb.din("cfix", [128, 4, 16]))
            elif s[0] == "odd_tail":
                dr[i] = dict(wout=kb.din("wout", [KC, 128, D]), ycv=kb.din("ycv", [1024, TOK]),
                             yhalo=kb.din("yhalo", [1024, HALO]), guv=kb.din("guv", [2048, TOK]),
                             pcols=kb.din("pcols", [128, 64]), cvw=kb.din("cvw", [128, 8, CVW]),
                             wsT=kb.din("wsT", [128, 8, 128]), sgb=kb.din("sgb", [128, 8, 128]))
        if has_tail:
            ident_d = kb.din("ident", [128, 128])
            maskup_d = kb.din("maskup", [128, 128])
        kb.setup_common()
        kb.alloc_ffn()
        if has_mix:
            kb.mixer_alloc()
        kb.load_x(x_in, g_in)
        if has_tail:
            kb.load_mixer_consts(ident_d, maskup_d)
        for i, s in enumerate(stages):
            if s[0] == "ffn":
                kb.plan_ffn(*dr[i])
            elif s[0] == "inproj_even":
                kb.plan_inproj(dr[i][0], 35)
            elif s[0] == "inproj_odd":
                kb.plan_inproj(dr[i][0], 32, ODD_ORDER)
            elif s[0] == "even_tail":
                kb.plan_outproj(dr[i]["wout"], dr[i]["poolw"])
            elif s[0] == "odd_tail":
                kb.plan_outproj(dr[i]["wout"])
        bout = Buf("dram_out")
        prev_mix = None
        for i, s in enumerate(stages):
            is_mix = s[0] != "ffn"
            if prev_mix is not None and prev_mix != is_mix:
                kb.P.barrier()
            prev_mix = is_mix
            if s[0] == "ffn":
                kb.emit_ffn(s[1], s[2])
            elif s[0] == "inproj_even":
                kb.emit_inproj_even(s[1], dr[i][1], bout)
            elif s[0] == "inproj_odd":
                kb.emit_inproj_odd(s[1], dr[i][1], dr[i][2], bout)
            elif s[0] == "even_tail":
                d = dr[i]
                kb.load_layer_params({"pcols": d["pcols"], "cfix": d["cfix"]})
                kb.emit_even_tail(s[1], d["ppool"], d["phalo"], d["yrw"])
            elif s[0] == "odd_tail":
                d = dr[i]
                kb.load_layer_params({k: d[k] for k in ("pcols", "cvw", "wsT", "sgb")})
                kb.emit_odd_tail(s[1], d["ycv"], d["yhalo"], d["guv"])
        kb.store_x(xs)
        kb.P.finish()
    _PROG_CACHE[key] = nc
    return nc


def build_rw_launch():
    key = ("rw",)
    if key in _PROG_CACHE:
        return _PROG_CACHE[key]
    nc = bass.Bass("TRN2", target_bir_lowering=False)
    with ExitStack() as st:
        RWB(nc, st).build()
    _PROG_CACHE[key] = nc
    return nc


N_CORES = 8
_DBG = None


def _run(nc, in_maps):
    res = run_bass_kernel_spmd(nc, in_maps, core_ids=list(range(N_CORES)))
    return res.results


def _f32(a):
    return np.ascontiguousarray(a, dtype=np.float32)


def _gcols(norm_g, layers):
    g = np.ones((2, 6, D), np.float32)
    for i, l in enumerate(layers):
        g[i] = norm_g[l]
    return _f32(g.reshape(2 * 6, KC, 128).transpose(2, 0, 1).reshape(128, 2 * 6 * KC))


def _colchunks(v):
    return v.reshape(8, 128).T


def kernel(**inp):
    inp = {k: np.asarray(v) for k, v in inp.items()}
    x = inp["x"]
    B = x.shape[0]
    ng = inp["norm_g"]
    xs = [_f32(x[c // 2, (c % 2) * TOK:(c % 2 + 1) * TOK].T) for c in range(N_CORES)]
    tw = {}

    def ffn_w(l, f):
        k = ("ffn", l, f)
        if k not in tw:
            tw[k] = (tile_w(inp["ffn_w1"][l, f]), tile_w(inp["ffn_w3"][l, f]), tile_w(inp["ffn_w2"][l, f]))
        return tw[k]

    cm = rw_consts()
    ident = cm["ident"]
    maskup = _f32(cm["mask_mt"][:, 128:256])

    def run_tok(stages, layers, extra):
        nc = build_tok_launch(stages)
        gc = _gcols(ng, layers)
        shared = {}
        for i, s in enumerate(stages):
            if s[0] == "ffn":
                w1, w3, w2 = ffn_w(layers[s[1]], s[2])
                shared[f"w1_{i}"], shared[f"w3_{i}"], shared[f"w2_{i}"] = w1, w3, w2
        maps = []
        for c in range(N_CORES):
            m = {"x_in": xs[c], "gcol": gc}
            m.update(shared)
            m.update(extra(c))
            maps.append(m)
        return _run(nc, maps)

    def even_in(e):
        return {"win": tile_w(inp["ev_w_in"][e], pad_to=35 * 128)}

    def odd_in(o):
        return {"win": tile_w(inp["od_w_in"][o])}

    def even_tail_in(e, pTs, yTs):
        poolw = _f32(np.stack([tile_w(inp["pool_w"][e][g]) for g in range(4)], 0))
        pc = np.zeros((128, 64), np.float32)
        pc[:, 0:8] = _colchunks(inp["pool_scale"][e])
        wout = tile_w(inp["ev_w_out"][e])

        def f(c):
            b, jt = c // 2, c % 2
            cf = np.zeros((4, 16), np.float32)
            for g in range(4):
                w = 2 ** (g + 1)
                cf[g] = 1.0 / (np.minimum(np.arange(16) + 1, w) if jt == 0 else w)
            ph = np.zeros((1024, HALO), np.float32) if jt == 0 else pTs[c - 1][0:1024, TOK - HALO:]
            yrw = np.concatenate([np.asarray(yTs[2 * b + jh])[:, jt * TOK:(jt + 1) * TOK] for jh in range(2)], axis=0)
            return {"wout": wout, "poolw": poolw, "ppool": _f32(pTs[c][0:1024]), "phalo": _f32(ph),
                    "yrw": np.ascontiguousarray(yrw), "pcols": pc, "cfix": _f32(np.broadcast_to(cf, (128, 4, 16))),
                    "ident": ident, "maskup": maskup}
        return f

    def odd_tail_in(o, ycvs, guvs):
        pc = np.zeros((128, 64), np.float32)
        pc[:, 16:24] = _colchunks(inp["cv_db"][o])
        pc[:, 24:32] = _colchunks(inp["cv_ln_g"][o])
        pc[:, 32:40] = _colchunks(inp["cv_ln_b"][o])
        pc[:, 40:48] = _colchunks(inp["sg_ln_g"][o])
        pc[:, 48:56] = _colchunks(inp["sg_ln_b"][o])
        cvw = _f32(inp["cv_dw"][o].reshape(CVW, 8, 128).transpose(2, 1, 0))
        wsT = _f32(inp["sg_ws"][o].transpose(2, 0, 1))
        sgb = _f32(np.broadcast_to(inp["sg_b"][o], (128, 8, 128)))
        wout = tile_w(inp["od_w_out"][o])

        def f(c):
            jt = c % 2
            yh = np.zeros((1024, HALO), np.float32) if jt == 0 else ycvs[c - 1][:, TOK - HALO:]
            return {"wout": wout, "ycv": _f32(ycvs[c]), "yhalo": _f32(yh), "guv": _f32(guvs[c]), "pcols": pc, "cvw": cvw,
                    "wsT": wsT, "sgb": sgb, "ident": ident, "maskup": maskup}
        return f

    def run_rw(e, pTs):
        nc = build_rw_launch()
        prm = {k: inp[k] for k in ("ev_mu", "rw_w0", "rw_w2", "rw_a0", "rw_a2", "rw_g2", "rw_kk", "rw_ka", "rw_rk",
                                   "rw_ln_g", "rw_ln_b")}
        maps = []
        for c in range(N_CORES):
            b, jh = c // 2, c % 2
            pT_b = np.concatenate([pTs[2 * b], pTs[2 * b + 1]], axis=1)
            maps.append(rw_core_inputs(pT_b, e, jh, prm))
        return [r["yT"] for r in _run(nc, maps)]

    ntag = [0]

    def upd(res):
        for c in range(N_CORES):
            xs[c] = _f32(res[c]["xs"])
        if _DBG is not None:
            _DBG(ntag[0], xs, res)
        ntag[0] += 1

    r = run_tok((("ffn", 0, 0), ("inproj_even", 0)), [0], lambda c: even_in(0))
    upd(r)
    pTs = [np.asarray(r[c]["pT"]) for c in range(N_CORES)]
    yTs = run_rw(0, pTs)
    ex = even_tail_in(0, pTs, yTs)
    r = run_tok((("even_tail", 0), ("ffn", 0, 1), ("ffn", 1, 0), ("inproj_odd", 1)), [0, 1],
                lambda c: {**ex(c), **odd_in(0)})
    upd(r)
    ex = odd_tail_in(0, [np.asarray(r[c]["ycv_o"]) for c in range(N_CORES)], [np.asarray(r[c]["guv_o"]) for c in range(N_CORES)])
    r = run_tok((("odd_tail", 0), ("ffn", 0, 1), ("ffn", 1, 0), ("inproj_even", 1)), [1, 2],
                lambda c: {**ex(c), **even_in(1)})
    upd(r)
    pTs = [np.asarray(r[c]["pT"]) for c in range(N_CORES)]
    yTs = run_rw(1, pTs)
    ex = even_tail_in(1, pTs, yTs)
    r = run_tok((("even_tail", 0), ("ffn", 0, 1), ("ffn", 1, 0), ("inproj_odd", 1)), [2, 3],
                lambda c: {**ex(c), **odd_in(1)})
    upd(r)
    ex = odd_tail_in(1, [np.asarray(r[c]["ycv_o"]) for c in range(N_CORES)], [np.asarray(r[c]["guv_o"]) for c in range(N_CORES)])
    r = run_tok((("odd_tail", 0), ("ffn", 0, 1)), [3], lambda c: ex(c))
    upd(r)
    out = np.empty_like(x)
    for c in range(N_CORES):
        out[c // 2, (c % 2) * TOK:(c % 2 + 1) * TOK] = xs[c].T
    return out
```
